# Optimizing a Trainium2 kernel written in Bass

```python
import jax, jax.numpy as jnp
from jax import lax
import numpy as np

D_MODEL = 2048
BATCH = 4
SEQ = 4096
DEPTH = 2

MEM_LEN = 256
HEAD_DIM = 64
DIL_GROUPS = ((128, 1), (512, 4), (2048, 16))
A_HEADS_PER_GROUP = 8
A_HEADS = A_HEADS_PER_GROUP * len(DIL_GROUPS)
BAND_BLOCK = 128
CHUNK = 128
B_GROUPS = 8
B_WIDTH = 1024
B_GROUP_DIM = B_WIDTH // B_GROUPS
C_HEADS = D_MODEL // HEAD_DIM
MOBA_BLOCK = 256
MOBA_TOPK = 3
X_HEADS = 4
X_HEAD_DIM = D_MODEL // X_HEADS
D_FF = ((8 * D_MODEL // 3 + 127) // 128) * 128
DEEPNORM_ALPHA = (2 * DEPTH) ** 0.25
DEEPNORM_BETA = (8 * DEPTH) ** -0.25
LN_EPS = 1e-5

A_QKV_WIDTH = 3 * A_HEADS * HEAD_DIM
MIX0_IN = A_QKV_WIDTH + 2 * B_WIDTH
MIX0_OUT = A_HEADS_PER_GROUP * HEAD_DIM + B_WIDTH
MIX1_IN = 3 * C_HEADS * HEAD_DIM
MIX1_OUT = C_HEADS * HEAD_DIM

kernel_name = 'hybrid_dilated_gmlp_moba_deepnorm'


def layer_norm(x, g, b):
    xf = x.astype(jnp.float32)
    mu = jnp.mean(xf, axis=-1, keepdims=True)
    var = jnp.mean(jnp.square(xf - mu), axis=-1, keepdims=True)
    y = (xf - mu) * lax.rsqrt(var + LN_EPS) * g.astype(jnp.float32) + b.astype(jnp.float32)
    return y.astype(x.dtype)


def deepnorm_residual(x, fx, g, b):
    return layer_norm(DEEPNORM_ALPHA * x + fx, g, b)


def swiglu_ffn(x, w_in, w_out):
    gate, up = jnp.split(x @ w_in, 2, axis=-1)
    return (jax.nn.silu(gate) * up) @ w_out


def dilated_window_attention(q, k, v, window, dilation):
    B, S, H, hd = q.shape
    L = S // dilation
    Lp = -(-L // BAND_BLOCK) * BAND_BLOCK
    nblk = Lp // BAND_BLOCK
    max_off = window // dilation

    def to_blocks(t):
        t = t.reshape(B, L, dilation, H, hd).transpose(0, 2, 3, 1, 4)
        t = jnp.pad(t, ((0, 0), (0, 0), (0, 0), (0, Lp - L), (0, 0)))
        return t.reshape(B, dilation, H, nblk, BAND_BLOCK, hd)

    def with_prev(t):
        prev = jnp.pad(t, ((0, 0), (0, 0), (0, 0), (1, 0), (0, 0), (0, 0)))[:, :, :, :-1]
        return jnp.concatenate([prev, t], axis=4)

    qb = to_blocks(q)
    kc = with_prev(to_blocks(k))
    vc = with_prev(to_blocks(v))
    s = jnp.einsum('bdhnqc,bdhnkc->bdhnqk', qb, kc).astype(jnp.float32) * (hd ** -0.5)
    qi = jnp.arange(BAND_BLOCK)[:, None]
    ki = jnp.arange(2 * BAND_BLOCK)[None, :]
    off = qi + BAND_BLOCK - ki
    band = (off >= 0) & (off <= max_off)
    has_prev = (jnp.arange(nblk) > 0)[:, None, None] | (ki >= BAND_BLOCK)[None]
    mask = band[None] & has_prev
    s = jnp.where(mask, s, -jnp.inf)
    lse = jax.nn.logsumexp(s, axis=-1)
    p = jnp.exp(s - lse[..., None]).astype(v.dtype)
    o = jnp.einsum('bdhnqk,bdhnkc->bdhnqc', p, vc)
    o = o.reshape(B, dilation, H, Lp, hd)[:, :, :, :L].transpose(0, 3, 1, 2, 4).reshape(B, S, H, hd)
    lse = lse.reshape(B, dilation, H, Lp)[:, :, :, :L].transpose(0, 3, 1, 2).reshape(B, S, H)
    return o, lse


def mixer_a(q, k, v):
    outs, lses = [], []
    for g, (window, dilation) in enumerate(DIL_GROUPS):
        sl = slice(g * A_HEADS_PER_GROUP, (g + 1) * A_HEADS_PER_GROUP)
        o, l = dilated_window_attention(q[:, :, sl], k[:, :, sl], v[:, :, sl], window, dilation)
        outs.append(o)
        lses.append(l)
    w = jax.nn.softmax(jnp.stack(lses, axis=0), axis=0).astype(q.dtype)
    return jnp.einsum('gbsh,gbshc->bshc', w, jnp.stack(outs, axis=0))


def mixer_b(u, v, ln_g, ln_b, w_s, b_s):
    Bsz, S, _ = u.shape
    u = jax.nn.gelu(u)
    v = layer_norm(jax.nn.gelu(v), ln_g, ln_b)
    nc = S // CHUNK
    vc = v.reshape(Bsz, nc, CHUNK, B_GROUPS, B_GROUP_DIM)
    tri = jnp.tril(jnp.ones((CHUNK, CHUNK), dtype=bool))
    w = jnp.where(tri[None], w_s, 0)
    mixed = jnp.einsum('gij,bnjgc->bnigc', w, vc) + b_s.T[None, None, :, :, None]
    return u * mixed.reshape(Bsz, S, B_WIDTH)


def moba_attention(q, k, v):
    B, S, H, hd = q.shape
    Sp = -(-S // MOBA_BLOCK) * MOBA_BLOCK
    nb = Sp // MOBA_BLOCK
    kt = min(MOBA_TOPK, nb)
    scale = hd ** -0.5

    def to_blocks(t):
        t = jnp.pad(t, ((0, 0), (0, Sp - S), (0, 0), (0, 0))).transpose(0, 2, 1, 3)
        return t.reshape(B * H, nb, MOBA_BLOCK, hd)

    qb, kb, vb = to_blocks(q), to_blocks(k), to_blocks(v)
    kmean = jnp.mean(kb.astype(jnp.float32), axis=2).astype(kb.dtype)
    gate = jnp.einsum('znqc,zmc->znqm', qb, kmean).astype(jnp.float32)
    past = jnp.arange(nb)[None, :] < jnp.arange(nb)[:, None]
    gate = jnp.where(past[None, :, None, :], gate, -jnp.inf)
    top_val, top_idx = lax.top_k(gate, kt)
    sel_ok = jnp.isfinite(top_val)
    causal = jnp.tril(jnp.ones((MOBA_BLOCK, MOBA_BLOCK), dtype=bool))

    def attend_block(args):
        z, qc, idx, ok, k_own, v_own = args
        k_sel = kb[z][idx]
        v_sel = vb[z][idx]
        s_sel = jnp.einsum('qc,qjkc->qjk', qc, k_sel).astype(jnp.float32) * scale
        s_sel = jnp.where(ok[:, :, None], s_sel, -jnp.inf).reshape(MOBA_BLOCK, kt * MOBA_BLOCK)
        s_own = jnp.where(causal, (qc @ k_own.T).astype(jnp.float32) * scale, -jnp.inf)
        p = jax.nn.softmax(jnp.concatenate([s_sel, s_own], axis=-1), axis=-1).astype(v_own.dtype)
        p_sel = p[:, :kt * MOBA_BLOCK].reshape(MOBA_BLOCK, kt, MOBA_BLOCK)
        p_own = p[:, kt * MOBA_BLOCK:]
        return jnp.einsum('qjk,qjkc->qc', p_sel, v_sel) + p_own @ v_own

    n_items = B * H * nb
    z_idx = jnp.repeat(jnp.arange(B * H), nb)
    out = lax.map(attend_block, (z_idx,
                                 qb.reshape(n_items, MOBA_BLOCK, hd),
                                 top_idx.reshape(n_items, MOBA_BLOCK, kt),
                                 sel_ok.reshape(n_items, MOBA_BLOCK, kt),
                                 kb.reshape(n_items, MOBA_BLOCK, hd),
                                 vb.reshape(n_items, MOBA_BLOCK, hd)))
    return out.reshape(B, H, Sp, hd).transpose(0, 2, 1, 3)[:, :S]


def token_mix_even(x, w_in, gmlp_ln_g, gmlp_ln_b, gmlp_w_s, gmlp_b_s, w_out):
    B, S, _ = x.shape
    h = x @ w_in
    qkv, u, v = jnp.split(h, [A_QKV_WIDTH, A_QKV_WIDTH + B_WIDTH], axis=-1)
    qkv = qkv.reshape(B, S, 3, A_HEADS, HEAD_DIM)
    a_out = mixer_a(qkv[:, :, 0], qkv[:, :, 1], qkv[:, :, 2]).reshape(B, S, A_HEADS_PER_GROUP * HEAD_DIM)
    b_out = mixer_b(u, v, gmlp_ln_g, gmlp_ln_b, gmlp_w_s, gmlp_b_s)
    return jnp.concatenate([a_out, b_out], axis=-1) @ w_out


def token_mix_odd(x, w_in, w_out):
    B, S, _ = x.shape
    qkv = (x @ w_in).reshape(B, S, 3, C_HEADS, HEAD_DIM)
    o = moba_attention(qkv[:, :, 0], qkv[:, :, 1], qkv[:, :, 2])
    return o.reshape(B, S, MIX1_OUT) @ w_out


def memory_cross_attention(x, mem, w_q, w_kv, w_o):
    B, S, _ = x.shape
    M = mem.shape[1]
    q = (x @ w_q).reshape(B, S, X_HEADS, X_HEAD_DIM)
    k, v = jnp.split(mem @ w_kv, 2, axis=-1)
    k = k.reshape(B, M, X_HEADS, X_HEAD_DIM)
    v = v.reshape(B, M, X_HEADS, X_HEAD_DIM)
    s = jnp.einsum('bshc,bmhc->bhsm', q, k).astype(jnp.float32) * (X_HEAD_DIM ** -0.5)
    p = jax.nn.softmax(s, axis=-1).astype(v.dtype)
    o = jnp.einsum('bhsm,bmhc->bshc', p, v).reshape(B, S, D_MODEL)
    return o @ w_o


def hybrid_layer(x, mem, p, even):
    x = deepnorm_residual(x, 0.5 * swiglu_ffn(x, *p['ffn1']), *p['ln1'])
    mixed = token_mix_even(x, *p['mix']) if even else token_mix_odd(x, *p['mix'])
    x = deepnorm_residual(x, mixed, *p['ln2'])
    x = deepnorm_residual(x, memory_cross_attention(x, mem, *p['mem']), *p['ln3'])
    x = deepnorm_residual(x, 0.5 * swiglu_ffn(x, *p['ffn2']), *p['ln4'])
    return x


def setup_inputs(seed: int = 0) -> dict:
    key = jax.random.key(seed)
    keys = iter(jax.random.split(key, 64))

    def normal(shape, scale):
        return jax.random.normal(next(keys), shape, jnp.float32) * scale

    def gain(n):
        return 1.0 + normal((n,), 0.02)

    def bias(n):
        return normal((n,), 0.02)

    inp = {'x': normal((BATCH, SEQ, D_MODEL), 1.0),
           'mem': normal((BATCH, MEM_LEN, D_MODEL), 1.0)}
    for i in range(DEPTH):
        p = 'l%d_' % i
        inp[p + 'ffn1_w_in'] = normal((D_MODEL, 2 * D_FF), D_MODEL ** -0.5)
        inp[p + 'ffn1_w_out'] = normal((D_FF, D_MODEL), D_FF ** -0.5 * DEEPNORM_BETA)
        inp[p + 'ln1_g'] = gain(D_MODEL)
        inp[p + 'ln1_b'] = bias(D_MODEL)
        if i % 2 == 0:
            inp[p + 'mix_w_in'] = normal((D_MODEL, MIX0_IN), D_MODEL ** -0.5)
            inp[p + 'gmlp_ln_g'] = gain(B_WIDTH)
            inp[p + 'gmlp_ln_b'] = bias(B_WIDTH)
            inp[p + 'gmlp_w_s'] = normal((B_GROUPS, CHUNK, CHUNK), CHUNK ** -0.5)
            inp[p + 'gmlp_b_s'] = 1.0 + normal((B_GROUPS, CHUNK), 0.02)
            inp[p + 'mix_w_out'] = normal((MIX0_OUT, D_MODEL), MIX0_OUT ** -0.5 * DEEPNORM_BETA)
        else:
            inp[p + 'mix_w_in'] = normal((D_MODEL, MIX1_IN), D_MODEL ** -0.5)
            inp[p + 'mix_w_out'] = normal((MIX1_OUT, D_MODEL), MIX1_OUT ** -0.5 * DEEPNORM_BETA)
        inp[p + 'ln2_g'] = gain(D_MODEL)
        inp[p + 'ln2_b'] = bias(D_MODEL)
        inp[p + 'mem_w_q'] = normal((D_MODEL, D_MODEL), D_MODEL ** -0.5)
        inp[p + 'mem_w_kv'] = normal((D_MODEL, 2 * D_MODEL), D_MODEL ** -0.5)
        inp[p + 'mem_w_o'] = normal((D_MODEL, D_MODEL), D_MODEL ** -0.5 * DEEPNORM_BETA)
        inp[p + 'ln3_g'] = gain(D_MODEL)
        inp[p + 'ln3_b'] = bias(D_MODEL)
        inp[p + 'ffn2_w_in'] = normal((D_MODEL, 2 * D_FF), D_MODEL ** -0.5)
        inp[p + 'ffn2_w_out'] = normal((D_FF, D_MODEL), D_FF ** -0.5 * DEEPNORM_BETA)
        inp[p + 'ln4_g'] = gain(D_MODEL)
        inp[p + 'ln4_b'] = bias(D_MODEL)
    return inp


def reference(x, mem,
              l0_ffn1_w_in, l0_ffn1_w_out, l0_ln1_g, l0_ln1_b,
              l0_mix_w_in, l0_gmlp_ln_g, l0_gmlp_ln_b, l0_gmlp_w_s, l0_gmlp_b_s, l0_mix_w_out,
              l0_ln2_g, l0_ln2_b, l0_mem_w_q, l0_mem_w_kv, l0_mem_w_o, l0_ln3_g, l0_ln3_b,
              l0_ffn2_w_in, l0_ffn2_w_out, l0_ln4_g, l0_ln4_b,
              l1_ffn1_w_in, l1_ffn1_w_out, l1_ln1_g, l1_ln1_b,
              l1_mix_w_in, l1_mix_w_out,
              l1_ln2_g, l1_ln2_b, l1_mem_w_q, l1_mem_w_kv, l1_mem_w_o, l1_ln3_g, l1_ln3_b,
              l1_ffn2_w_in, l1_ffn2_w_out, l1_ln4_g, l1_ln4_b):
    layers = [
        dict(ffn1=(l0_ffn1_w_in, l0_ffn1_w_out), ln1=(l0_ln1_g, l0_ln1_b),
             mix=(l0_mix_w_in, l0_gmlp_ln_g, l0_gmlp_ln_b, l0_gmlp_w_s, l0_gmlp_b_s, l0_mix_w_out),
             ln2=(l0_ln2_g, l0_ln2_b), mem=(l0_mem_w_q, l0_mem_w_kv, l0_mem_w_o),
             ln3=(l0_ln3_g, l0_ln3_b), ffn2=(l0_ffn2_w_in, l0_ffn2_w_out), ln4=(l0_ln4_g, l0_ln4_b)),
        dict(ffn1=(l1_ffn1_w_in, l1_ffn1_w_out), ln1=(l1_ln1_g, l1_ln1_b),
             mix=(l1_mix_w_in, l1_mix_w_out),
             ln2=(l1_ln2_g, l1_ln2_b), mem=(l1_mem_w_q, l1_mem_w_kv, l1_mem_w_o),
             ln3=(l1_ln3_g, l1_ln3_b), ffn2=(l1_ffn2_w_in, l1_ffn2_w_out), ln4=(l1_ln4_g, l1_ln4_b)),
    ]
    for i in range(DEPTH):
        x = hybrid_layer(x, mem, layers[i], i % 2 == 0)
    return x
```

```python
import contextlib
import numpy as np
import ml_dtypes
import concourse.bass as bass
import concourse.mybir as mybir
from concourse.bass_utils import run_bass_kernel_spmd

F32 = mybir.dt.float32
BF16 = mybir.dt.bfloat16
AF = mybir.ActivationFunctionType
ALU = mybir.AluOpType
AX = mybir.AxisListType

D = 2048
NCH = 16
DFF = 5504
NHC = 43
ALPHA = float(4 ** 0.25)
EPS = 1e-5
NEG = -30000.0
TT = 1024
SEM_ROLL = 30000
DMA_SLOTS = 8
ARENA = 52800


class _Op:
    __slots__ = ("eng", "fn", "deps", "signal", "sig", "dma", "idx")

    def __init__(self, eng, fn, dma):
        self.eng = eng
        self.fn = fn
        self.dma = dma
        self.deps = set()
        self.signal = False
        self.sig = None
        self.idx = None


class Sched:
    def __init__(self, nc):
        self.nc = nc
        self.ops = []
        self.state = {}
        self.children = {}
        self.bar = None
        self.bar_seen = {}
        self.epoch = 0
        self.last_of = {}
        self.last_dmas = {}
        self.bank_last = {}

    def _conf(self, k):
        out = []
        for i in range(1, len(k) + 1):
            p = k[:i]
            if p in self.state:
                out.append(p)
        for c in self.children.get(k, ()):
            out.append(c)
        return out

    def _touch(self, k):
        if k not in self.state:
            self.state[k] = [None, []]
            for i in range(1, len(k)):
                self.children.setdefault(k[:i], set()).add(k)

    def barrier(self):
        b = set(self.last_of.values())
        for q, l in self.last_dmas.items():
            b.update(l)
        self.bar = b
        self.epoch += 1

    def op(self, eng, fn, reads=(), writes=(), dma=False):
        o = _Op(eng, fn, dma)
        o.idx = len(self.ops)
        self.ops.append(o)
        reads = [k if isinstance(k, tuple) else (k,) for k in reads]
        writes = [k if isinstance(k, tuple) else (k,) for k in writes]
        if self.bar is not None and self.bar_seen.get(eng, 0) < self.epoch:
            o.deps |= self.bar
            self.bar_seen[eng] = self.epoch
        for k in list(reads) + list(writes):
            if k[0] == "ps":
                bl = self.bank_last.setdefault(k[1], {})
                for e2, i2 in bl.items():
                    if e2 != eng:
                        o.deps.add(i2)
                bl[eng] = o.idx
        for k in reads:
            self._touch(k)
            for c in self._conf(k):
                w = self.state[c][0]
                if w is not None:
                    o.deps.add(w)
        for k in writes:
            self._touch(k)
            for c in self._conf(k):
                st = self.state[c]
                if st[0] is not None:
                    o.deps.add(st[0])
                o.deps.update(st[1])
        for k in reads:
            self.state[k][1].append(o.idx)
        for k in writes:
            self.state[k] = [o.idx, []]
            for c in self.children.get(k, ()):
                self.state[c] = [o.idx, []]
        o.deps.discard(o.idx)
        if dma:
            l = self.last_dmas.setdefault(eng, [])
            l.append(o.idx)
            if len(l) > DMA_SLOTS:
                l.pop(0)
        else:
            self.last_of[eng] = o.idx
        return o

    def emit(self, final_wait_eng="sync"):
        nc = self.nc
        ops = self.ops
        for o in ops:
            keep = set()
            for d in o.deps:
                p = ops[d]
                if p.eng == o.eng and not p.dma and not o.dma and o.eng == "tensor":
                    continue
                keep.add(d)
            o.deps = keep
            for d in keep:
                ops[d].signal = True
        cnt, dcnt, n_sems = {}, {}, {}
        for o in ops:
            if o.dma:
                i = dcnt.get(o.eng, 0)
                dcnt[o.eng] = i + 1
                o.sig = ("dma", o.eng, i % DMA_SLOTS, 16 * (i // DMA_SLOTS + 1))
            elif o.signal:
                i = cnt.get(o.eng, 0)
                cnt[o.eng] = i + 1
                o.sig = ("cmp", o.eng, i // SEM_ROLL, i % SEM_ROLL + 1)
                n_sems[o.eng] = i // SEM_ROLL + 1
        stack = contextlib.ExitStack()
        sems = {}
        for e, n in n_sems.items():
            for j in range(n):
                sems[("cmp", e, j)] = stack.enter_context(nc.semaphore("s_%s_%d" % (e, j)))
        for e in dcnt:
            for j in range(min(DMA_SLOTS, dcnt[e])):
                sems[("dma", e, j)] = stack.enter_context(nc.semaphore("d_%s_%d" % (e, j)))
        streams, seen, last_dma = {}, {}, {}
        relay_sem = stack.enter_context(nc.semaphore("relay"))
        relay_n = 0
        for o in ops:
            st = streams.setdefault(o.eng, [])
            sn = seen.setdefault(o.eng, {})
            need = {}
            for d in o.deps:
                p = ops[d]
                sk, v = p.sig[:3], p.sig[3]
                if need.get(sk, 0) < v:
                    need[sk] = v
            if o.dma:
                sk, v = o.sig[:3], o.sig[3] - 16
                if v > 0 and need.get(sk, 0) < v:
                    need[sk] = v
            for sk, v in need.items():
                if sn.get(sk, 0) >= v:
                    continue
                sn[sk] = v
                if o.eng == "vector" and sk[0] == "dma" and sk[1] == "gpsimd":
                    ast = streams.setdefault("scalar", [])
                    asn = seen.setdefault("scalar", {})
                    if asn.get(sk, 0) < v:
                        asn[sk] = v
                        ast.append(("wait", sems[sk], v))
                    relay_n += 1
                    ast.append(("inc", relay_sem, 1))
                    st.append(("wait", relay_sem, relay_n))
                    continue
                st.append(("wait", sems[sk], v))
            if o.dma:
                st.append(("dma", o.fn, sems[o.sig[:3]]))
                last_dma[(o.eng, o.sig[2])] = (sems[o.sig[:3]], o.sig[3])
            elif o.signal:
                st.append(("sig", o.fn, sems[o.sig[:3]]))
            else:
                st.append(("op", o.fn, None))
        fst = streams.setdefault(final_wait_eng, [])
        for (e, slot), (sem, v) in last_dma.items():
            fst.append(("wait", sem, v))

        self.streams = streams
        import os
        if os.environ.get("DUMP"):
            for en, stl in streams.items():
                print("ENGINE", en)
                for kind, a, b in stl[:int(os.environ.get("DUMP"))]:
                    print("   ", kind, (a.name if hasattr(a, "name") else ""), b if not callable(b) else "", getattr(b, "name", ""))

        def run(engname, eng):
            for kind, a, b in streams.get(engname, []):
                if kind == "wait":
                    eng.wait_ge(a, b)
                elif kind == "inc":
                    eng.sem_inc(a, b)
                elif kind == "dma":
                    a(eng).then_inc(b, 16)
                elif kind == "sig":
                    a(eng).then_inc(b, 1)
                else:
                    a(eng)

        with stack:
            with nc.Block() as block:
                @block.sync
                def _(e):
                    run("sync", e)

                @block.tensor
                def _(e):
                    run("tensor", e)

                @block.vector
                def _(e):
                    run("vector", e)

                @block.scalar
                def _(e):
                    run("scalar", e)

                @block.gpsimd
                def _(e):
                    run("gpsimd", e)


class Buf:
    __slots__ = ("ap", "key")

    def __init__(self, ap, key):
        self.ap = ap
        self.key = key

    def k(self, *sub):
        return self.key + tuple(sub)


def lay_units(W, unit=256):
    K, F = W.shape
    kc = K // 128
    u = F // unit
    return np.ascontiguousarray(W.reshape(kc, 128, u, unit).transpose(2, 1, 0, 3)).reshape(u, 128, kc * unit)


def lay_ffn_in(W):
    g = W[:, :DFF].reshape(NCH, 128, NHC, 128)
    u = W[:, DFF:].reshape(NCH, 128, NHC, 128)
    gu = np.concatenate([g, u], axis=3)
    return np.ascontiguousarray(gu.transpose(2, 1, 0, 3)).reshape(NHC, 128, NCH * 256)


def lay_vec(v):
    return np.ascontiguousarray(v.reshape(-1, 128).T)


def build_consts():
    p = np.arange(128)
    k = p[:, None]
    q = p[None, :]
    ident = (k == q).astype(np.float32)
    bf = [ident]
    for d in (1, 4, 16):
        comb = ((q - k) % d == 0)
        first = comb & (q >= k)
        mid = comb
        last = comb & (q <= k)
        for m in (first, mid, last):
            bf.append(np.where(m, 0.0, NEG).astype(np.float32))
    tri = np.where(k <= q, 0.0, NEG).astype(np.float32)
    z = np.zeros((128, 128), np.float32)
    full = np.full((128, 128), NEG, np.float32)
    bf.append(np.concatenate([tri, z], 1))
    bf.append(np.concatenate([full, tri], 1))
    E = np.zeros((128, 16 * 128), np.float32)
    for m in range(16):
        E[m, m * 128:(m + 1) * 128] = 1.0
    bf.append(E)
    bf.append((k <= q).astype(np.float32))
    bf.append(np.ones((128, 128), np.float32))
    cb = np.concatenate(bf, 1)
    cf = np.concatenate([ident, np.full((128, 128), 1.0 / D, np.float32)], 1)
    return cb, cf


CB_IDENT = 0
CB_DMASK = 128
CB_CAUS = CB_DMASK + 9 * 128
CB_E = CB_CAUS + 512
CB_TRI = CB_E + 2048
CB_ONES = CB_TRI + 128
CB_N = CB_ONES + 128
CF_IDENT = 0
CF_ONES = 128
CF_N = 256


class Prog:
    def __init__(self, nc, NT, NP):
        self.nc = nc
        self.NT = NT
        self.NP = NP
        self.S = Sched(nc)
        self.es = contextlib.ExitStack()
        self.A = self.es.enter_context(nc.sbuf_tensor("arena", [128, ARENA], F32))
        self.ps = [Buf(self.es.enter_context(nc.psum_tensor("ps%d" % i, [128, 512], F32))[:], ("ps", i))
                   for i in range(8)]
        self.off = 0
        self.uid = 0
        self.pi = 0
        self.din = {}

    def dram_in(self, name, shape, dt=F32):
        t = self.nc.dram_tensor(name, list(shape), dt, kind="ExternalInput").ap()
        self.din[name] = (tuple(shape), dt)
        return Buf(t, (name,))

    def dram_out(self, name, shape, dt=F32):
        t = self.nc.dram_tensor(name, list(shape), dt, kind="ExternalOutput").ap()
        return Buf(t, (name,))

    def dram_tmp(self, name, shape, dt=F32):
        t = self.nc.dram_tensor(name, list(shape), dt).ap()
        return Buf(t, (name,))

    def f32(self, n, name="f"):
        o = self.off
        self.off += n
        assert self.off <= ARENA, (self.off, name)
        self.uid += 1
        return Buf(self.A[:, o:o + n], ("%s%d" % (name, self.uid),))

    def bf(self, n, name="b"):
        assert n % 2 == 0
        o = self.off
        self.off += n // 2
        assert self.off <= ARENA, (self.off, name)
        self.uid += 1
        return Buf(self.A[:, o:o + n // 2].bitcast(BF16), ("%s%d" % (name, self.uid),))

    def mark(self):
        return self.off

    def reset(self, m):
        self.S.barrier()
        self.off = m

    def psum(self, banks=None):
        banks = banks or range(8)
        b = banks[self.pi % len(banks)]
        self.pi += 1
        return self.ps[b]

    def mm(self, out_ap, lhsT, rhs, start, stop, reads, writes, skip=False):
        if skip:
            fn = lambda e: e.matmul(out_ap, lhsT=lhsT, rhs=rhs, start=start, stop=stop, skip_group_check=True)
        else:
            fn = lambda e: e.matmul(out_ap, lhsT=lhsT, rhs=rhs, start=start, stop=stop)
        self.S.op("tensor", fn, reads=reads, writes=writes)

    def act(self, out_ap, in_ap, func, reads, writes, bias=None, scale=None, accum=None):
        kw = {}
        if bias is not None:
            kw["bias"] = bias
        if scale is not None:
            kw["scale"] = scale
        if accum is not None:
            kw["accum_out"] = accum
        self.S.op("scalar", lambda e: e.activation(out=out_ap, in_=in_ap, func=func, **kw), reads=reads, writes=writes)

    def tt(self, out_ap, a, b, op, reads, writes, eng="vector"):
        self.S.op(eng, lambda e: e.tensor_tensor(out=out_ap, in0=a, in1=b, op=op), reads=reads, writes=writes)

    def ts(self, out_ap, a, s1, s2, op0, op1, reads, writes, eng="vector"):
        if op1 is None:
            self.S.op(eng, lambda e: e.tensor_scalar(out=out_ap, in0=a, scalar1=s1, scalar2=None, op0=op0),
                      reads=reads, writes=writes)
        else:
            self.S.op(eng, lambda e: e.tensor_scalar(out=out_ap, in0=a, scalar1=s1, scalar2=s2, op0=op0, op1=op1),
                      reads=reads, writes=writes)

    def rsqrt_eps(self, ap, key):
        self.ts(ap, ap, EPS, None, ALU.add, None, [key], [key])
        self.act(ap, ap, AF.Sqrt, [key], [key])
        self.S.op("vector", lambda e: e.reciprocal(out=ap, in_=ap), reads=[key], writes=[key])

    def stt(self, out_ap, a, s, b, op0, op1, reads, writes, eng="vector"):
        self.S.op(eng, lambda e: e.scalar_tensor_tensor(out=out_ap, in0=a, scalar=s, in1=b, op0=op0, op1=op1),
                  reads=reads, writes=writes)

    def cp(self, out_ap, in_ap, reads, writes, eng="vector"):
        if eng == "scalar":
            self.S.op("scalar", lambda e: e.copy(out=out_ap, in_=in_ap), reads=reads, writes=writes)
        else:
            self.S.op(eng, lambda e: e.tensor_copy(out=out_ap, in_=in_ap), reads=reads, writes=writes)

    def dma(self, out_ap, in_ap, reads, writes, eng="sync"):
        self.S.op(eng, lambda e: e.dma_start(out=out_ap, in_=in_ap), reads=reads, writes=writes, dma=True)

    def load_consts(self, cb_d, cf_d):
        self.cb = self.bf(CB_N, "cb")
        self.cf = self.f32(CF_N, "cf")
        self.dma(self.cb.ap, cb_d.ap, [cb_d.key], [self.cb.key], eng="gpsimd")
        self.dma(self.cf.ap, cf_d.ap, [cf_d.key], [self.cf.key])
        self.ident_bf = self.cb.ap[:, CB_IDENT:CB_IDENT + 128]
        self.ones_bf = self.cb.ap[:, CB_ONES:CB_ONES + 128]
        self.identF = self.cf.ap[:, CF_IDENT:CF_IDENT + 128]
        self.onesF = self.cf.ap[:, CF_ONES:CF_ONES + 128]

    def stream(self, wd, units, slots, compute, depth=None):
        depth = depth or (len(slots) - 1)
        n = len(units)
        cnt = getattr(self, "_sc", 0)

        def load(i):
            sl = slots[(cnt + i) % len(slots)]
            self.dma(sl.ap[:, 0:wd.ap.shape[2]], wd.ap[units[i]], [wd.key], [sl.key], eng="gpsimd")
            return sl
        loaded = {}
        for i in range(min(depth, n)):
            loaded[i] = load(i)
        for i in range(n):
            if i + depth < n:
                loaded[i + depth] = load(i + depth)
            compute(i, units[i], loaded.pop(i))
        self._sc = cnt + n

    def layernorm(self, z, xb, g_ap, b_ap, gk):
        z3 = z.ap.rearrange("p (c t) -> p c t", c=NCH)
        x3 = xb.ap.rearrange("p (c t) -> p c t", c=NCH)
        sq = [self.f32(512, "sq") for _ in range(2)]
        tb = [self.f32(512, "tb") for _ in range(2)]
        mean = self.f32(512, "mean")
        rstd = self.f32(512, "rstd")
        for s in range(2):
            sl = slice(s * 512, (s + 1) * 512)
            pm = self.ps[4 + 2 * s]
            pe = self.ps[5 + 2 * s]
            for c in range(NCH):
                self.mm(pm.ap, self.onesF, z3[:, c, sl], c == 0, c == NCH - 1, [self.cf.key, z.k(c, s)], [pm.key])
            for c in range(NCH):
                q = sq[c % 2]
                self.act(q.ap, z3[:, c, sl], AF.Square, [z.k(c, s)], [q.key])
                self.mm(pe.ap, self.onesF, q.ap, c == 0, c == NCH - 1, [self.cf.key, q.key], [pe.key])
            self.cp(mean.ap, pm.ap, [pm.key], [mean.key])
            self.tt(rstd.ap, mean.ap, mean.ap, ALU.mult, [mean.key], [rstd.key])
            self.tt(rstd.ap, pe.ap, rstd.ap, ALU.subtract, [pe.key, rstd.key], [rstd.key])
            self.rsqrt_eps(rstd.ap, rstd.key)
            for c in range(NCH):
                t = tb[c % 2]
                self.tt(t.ap, z3[:, c, sl], mean.ap, ALU.subtract, [z.k(c, s), mean.key], [t.key])
                self.tt(t.ap, t.ap, rstd.ap, ALU.mult, [t.key, rstd.key], [t.key])
                self.act(z3[:, c, sl], t.ap, AF.Identity, [t.key, gk], [z.k(c, s)],
                         bias=b_ap[:, c:c + 1], scale=g_ap[:, c:c + 1])
                self.act(x3[:, c, sl], t.ap, AF.Identity, [t.key, gk], [xb.k(c, s)],
                         bias=b_ap[:, c:c + 1], scale=g_ap[:, c:c + 1])

    def prescale(self, z):
        z3 = z.ap.rearrange("p (c t) -> p c t", c=NCH)
        for c in range(NCH):
            eng = "scalar" if c % 2 else "gpsimd"
            if eng == "scalar":
                self.S.op("scalar", lambda e, c=c: e.mul(out=z3[:, c, :], in_=z3[:, c, :], mul=ALPHA),
                          reads=[z.k(c)], writes=[z.k(c)])
            else:
                self.ts(z3[:, c, :], z3[:, c, :], ALPHA, None, ALU.mult, None, [z.k(c)], [z.k(c)], eng="vector")

    def ffn(self, z, xb, win, wout, G=3):
        z3 = z.ap.rearrange("p (c t) -> p c t", c=NCH)
        x3 = xb.ap.rearrange("p (c t) -> p c t", c=NCH)
        self.prescale(z)
        wst = [self.bf(4096, "wst") for _ in range(4)]
        wos = [self.bf(2048, "wos") for _ in range(2 * G)]
        hb = [[self.bf(1024, "h") for _ in range(G)] for _ in range(2)]
        sg = [self.bf(512, "sg") for _ in range(2)]
        groups = [list(range(a, min(a + G, NHC))) for a in range(0, NHC, G)]
        cnt = [0]
        pend = []
        wi = [0]

        def load_in(j):
            sl = wst[j % 4]
            self.dma(sl.ap, win.ap[j], [win.key], [sl.key], eng="gpsimd")

        def load_out(j, gi):
            sl = wos[(gi % 2) * G + (j % G)]
            self.dma(sl.ap, wout.ap[j], [wout.key], [sl.key], eng="gpsimd")
            return sl

        def phase1(j, h):
            w3 = wst[j % 4].ap.rearrange("p (k n) -> p k n", k=NCH)
            for s in range(2):
                sl = slice(s * 512, (s + 1) * 512)
                pg, pu = self.ps[2 * s], self.ps[2 * s + 1]
                for kc in range(NCH):
                    self.mm(pg.ap, w3[:, kc, 0:128], x3[:, kc, sl], kc == 0, kc == NCH - 1,
                            [wst[j % 4].key, xb.k(kc, s)], [pg.key])
                for kc in range(NCH):
                    self.mm(pu.ap, w3[:, kc, 128:256], x3[:, kc, sl], kc == 0, kc == NCH - 1,
                            [wst[j % 4].key, xb.k(kc, s)], [pu.key])
                self.act(sg[s].ap, pg.ap, AF.Silu, [pg.key], [sg[s].key])
                self.tt(h.ap[:, sl], sg[s].ap, pu.ap, ALU.mult, [sg[s].key, pu.key], [h.k(s)])

        def phase2(js, hs, wsl):
            for f in range(NCH):
                for s in range(2):
                    sl = slice(s * 512, (s + 1) * 512)
                    po = self.ps[4 + cnt[0] % 4]
                    cnt[0] += 1
                    for i, j in enumerate(js):
                        self.mm(po.ap, wsl[i].ap[:, f * 128:(f + 1) * 128], hs[i].ap[:, sl], i == 0, i == len(js) - 1,
                                [wsl[i].key, hs[i].k(s)], [po.key])
                    self.stt(z3[:, f, sl], po.ap, 0.5, z3[:, f, sl], ALU.mult, ALU.add,
                             [po.key, z.k(f, s)], [z.k(f, s)])

        for j in range(3):
            load_in(j)
        prev = None
        for gi, js in enumerate(groups):
            wsl = [load_out(j, gi) for j in js]
            for i, j in enumerate(js):
                if j + 3 < NHC:
                    load_in(j + 3)
                phase1(j, hb[gi % 2][i])
            if prev is not None:
                phase2(*prev)
            prev = (js, hb[gi % 2][:len(js)], wsl)
        phase2(*prev)

    def proj_fm(self, wd, units, KC, rhs, rkeys, evac, slots, ntile=2, nw=512):
        def compute(i, u, sl):
            w3 = sl.ap[:, 0:KC * 256].rearrange("p (k n) -> p k n", k=KC)
            for half in range(2):
                for s in range(ntile):
                    po = self.psum()
                    for kc in range(KC):
                        self.mm(po.ap[:, 0:nw], w3[:, kc, half * 128:(half + 1) * 128], rhs(kc, s), kc == 0, kc == KC - 1,
                                [sl.key, rkeys(kc, s)], [po.key])
                    evac(i, 2 * i + half, s, po)
        self.stream(wd, units, slots, compute)

    def proj_tm(self, wd, units, KC, lhsT, lkeys, evac, slots, ntb):
        def compute(i, u, sl):
            w3 = sl.ap[:, 0:KC * 256].rearrange("p (k n) -> p k n", k=KC)
            for tb in range(ntb):
                po = self.psum()
                for kc in range(KC):
                    self.mm(po.ap[:, 0:256], lhsT(kc, tb), w3[:, kc, :], kc == 0, kc == KC - 1,
                            [sl.key, lkeys(kc, tb)], [po.key])
                evac(i, tb, po)
        self.stream(wd, units, slots, compute)

    def accum_z_evac(self, z, scale):
        z3 = z.ap.rearrange("p (c t) -> p c t", c=NCH)

        def evac(i, fc, s, po):
            sl = slice(s * 512, (s + 1) * 512)
            self.stt(z3[:, fc, sl], po.ap, scale, z3[:, fc, sl], ALU.mult, ALU.add, [po.key, z.k(fc, s)], [z.k(fc, s)])
        return evac

    def gelu(self, out_ap, po, n, tmp, wkeys):
        t = tmp
        self.act(t.ap[:, 0:n], po.ap[:, 0:n], AF.Square, [po.key], [t.key])
        self.ts(t.ap[:, 0:n], t.ap[:, 0:n], 0.044715, 1.0, ALU.mult, ALU.add, [t.key], [t.key])
        self.tt(t.ap[:, 0:n], t.ap[:, 0:n], po.ap[:, 0:n], ALU.mult, [t.key, po.key], [t.key])
        self.act(t.ap[:, 0:n], t.ap[:, 0:n], AF.Sigmoid, [t.key], [t.key], scale=1.5957691216057308)
        self.tt(out_ap, t.ap[:, 0:n], po.ap[:, 0:n], ALU.mult, [t.key, po.key], wkeys)

    def load_x(self, x_d, t0, z, xb):
        z3 = z.ap.rearrange("p (c t) -> p c t", c=NCH)
        x3 = xb.ap.rearrange("p (c t) -> p c t", c=NCH)
        xin = [self.f32(D, "xin") for _ in range(2)]
        import os
        for tb in range(int(os.environ.get("LXN", TT // 128))):
            xi = xin[tb % 2]
            self.dma(xi.ap, x_d.ap[t0 + tb * 128:t0 + (tb + 1) * 128, :], [x_d.key], [xi.key])
            for c4 in range(4):
                po = self.psum()
                for cc in range(4):
                    c = c4 * 4 + cc
                    self.S.op("tensor", lambda e, po=po, cc=cc, c=c, xi=xi: e.matmul(po.ap[:, cc * 128:(cc + 1) * 128], lhsT=xi.ap[:, c * 128:(c + 1) * 128], rhs=self.identF, start=True, stop=True),
                        reads=[xi.key, self.cf.key], writes=[po.key])
                tsl = slice(tb * 128, (tb + 1) * 128)
                for cc in range(4):
                    c = c4 * 4 + cc
                    pslc = po.ap[:, cc * 128:(cc + 1) * 128]
                    import os
                    LX = os.environ.get("LX", "")
                    e1 = ("vector" if cc % 2 else "scalar")
                    e2 = ("scalar" if cc % 2 else "vector")
                    if LX == "v":
                        e1 = e2 = "vector"
                    if LX == "s":
                        e1 = e2 = "scalar"
                    if LX == "vz":
                        e1, e2 = "vector", "scalar"
                    if LX == "vx":
                        e1, e2 = "scalar", "vector"
                    self.cp(z3[:, c, tsl], pslc, [po.key], [z.k(c, tb // 4)], eng=e1)
                    self.cp(x3[:, c, tsl], pslc, [po.key], [xb.k(c, tb // 4)], eng=e2)

    def store_out(self, out_d, t0, z):
        z3 = z.ap.rearrange("p (c t) -> p c t", c=NCH)
        xo = [self.f32(D, "xo") for _ in range(2)]
        for tb in range(TT // 128):
            o = xo[tb % 2]
            for c4 in range(4):
                po = self.psum()
                for cc in range(4):
                    c = c4 * 4 + cc
                    self.S.op("tensor", lambda e, po=po, cc=cc, c=c, tb=tb: e.matmul(po.ap[:, cc * 128:(cc + 1) * 128], lhsT=z3[:, c, tb * 128:(tb + 1) * 128], rhs=self.identF, start=True, stop=True),
                        reads=[z.k(c, tb // 4), self.cf.key], writes=[po.key])
                self.cp(o.ap[:, c4 * 512:(c4 + 1) * 512], po.ap, [po.key], [o.k(c4)], eng=("scalar" if c4 % 2 else "vector"))
            self.dma(out_d.ap[t0 + tb * 128:t0 + (tb + 1) * 128, :], o.ap, [o.key], [out_d.k(t0, tb)])


def load_ln(P, lnp_d, n):
    t = P.f32(32 * n, "lnp")
    P.dma(t.ap, lnp_d.ap, [lnp_d.key], [t.key])
    return t


def ln_aps(t, i):
    return t.ap[:, 32 * i:32 * i + 16], t.ap[:, 32 * i + 16:32 * i + 32]


def mixin0(P, xb, t0, W, gm, QT, KT, V, BO, upto=9):
    x3 = xb.ap.rearrange("p (c t) -> p c t", c=NCH)
    slots = [P.bf(4096, "ws") for _ in range(4)]
    stg = [P.bf(512, "stg") for _ in range(4)]
    sc = [0]
    rhs = lambda kc, s: x3[:, kc, s * 512:(s + 1) * 512]
    rk = lambda kc, s: xb.k(kc, s)

    def evac_to(dst):
        def evac(i, fc, s, po):
            st = stg[sc[0] % 4]
            sc[0] += 1
            P.cp(st.ap, po.ap, [po.key], [st.key], eng=("scalar" if sc[0] % 2 else "vector"))
            P.dma(dst.ap[fc, :, t0 + s * 512:t0 + (s + 1) * 512], st.ap, [st.key], [dst.k(fc, t0, s)])
        return evac
    P.proj_fm(W, list(range(0, 6)), NCH, rhs, rk, evac_to(QT), slots)
    P.proj_fm(W, list(range(6, 12)), NCH, rhs, rk, evac_to(KT), slots)
    if upto < 5:
        return
    vbuf = P.bf(8 * 1536, "vbuf")
    vb3 = vbuf.ap.rearrange("p (b c) -> p b c", b=8)

    def evac_v(i, tb, po):
        P.cp(vb3[:, tb, i * 256:(i + 1) * 256], po.ap[:, 0:256], [po.key], [vbuf.k(tb, i)],
             eng=("scalar" if tb % 2 else "vector"))
    lhs = lambda kc, tb: x3[:, kc, tb * 128:(tb + 1) * 128]
    lk = lambda kc, tb: xb.k(kc, tb // 4)
    P.proj_tm(W, list(range(12, 18)), NCH, lhs, lk, evac_v, slots, 8)
    P.dma(V.ap[t0:t0 + TT, :].rearrange("(b p) c -> p b c", p=128), vb3, [vbuf.key], [V.k(t0)])
    if upto < 6:
        return
    uT = P.bf(8 * 1024, "uT")
    u3 = uT.ap.rearrange("p (c t) -> p c t", c=8)
    gt = [P.f32(512, "gt") for _ in range(3)]
    gc = [0]

    def evac_u(i, fc, s, po):
        t = gt[gc[0] % 3]
        gc[0] += 1
        P.gelu(u3[:, fc, s * 512:(s + 1) * 512], po, 512, t, [uT.k(fc, s)])
    P.proj_fm(W, list(range(18, 22)), NCH, rhs, rk, evac_u, slots)
    gv = P.f32(8 * 1024, "gv")
    gv3 = gv.ap.rearrange("p (b c) -> p b c", b=8)

    def evac_g(i, tb, po):
        t = gt[gc[0] % 3]
        gc[0] += 1
        P.gelu(gv3[:, tb, i * 256:(i + 1) * 256], po, 256, t, [gv.k(tb, i)])
    P.proj_tm(W, list(range(22, 26)), NCH, lhs, lk, evac_g, slots, 8)
    if upto < 7:
        return
    boT = P.bf(8 * 1024, "boT")
    bo3 = boT.ap.rearrange("p (c t) -> p c t", c=8)
    st2 = [P.f32(8, "st") for _ in range(2)]
    vn = [P.bf(1024, "vn") for _ in range(2)]
    junk = P.f32(1024, "junk")
    for tb in range(8):
        st = st2[tb % 2]
        g = gv3[:, tb, :]
        P.act(junk.ap, g, AF.Identity, [gv.k(tb)], [junk.key, st.k(0)], accum=st.ap[:, 0:1])
        P.act(junk.ap, g, AF.Square, [gv.k(tb)], [junk.key, st.k(1)], accum=st.ap[:, 1:2])
        P.ts(st.ap[:, 0:2], st.ap[:, 0:2], 1.0 / 1024, None, ALU.mult, None, [st.k(0), st.k(1)], [st.k(0), st.k(1)])
        P.tt(st.ap[:, 2:3], st.ap[:, 0:1], st.ap[:, 0:1], ALU.mult, [st.k(0)], [st.k(2)])
        P.tt(st.ap[:, 2:3], st.ap[:, 1:2], st.ap[:, 2:3], ALU.subtract, [st.k(1), st.k(2)], [st.k(2)])
        P.rsqrt_eps(st.ap[:, 2:3], st.k(2))
        P.stt(st.ap[:, 3:4], st.ap[:, 0:1], -1.0, st.ap[:, 2:3], ALU.mult, ALU.mult, [st.k(0), st.k(2)], [st.k(3)])
        P.act(g, g, AF.Identity, [gv.k(tb), st.k(2), st.k(3)], [gv.k(tb)], bias=st.ap[:, 3:4], scale=st.ap[:, 2:3])
        P.tt(g, g, gm["g_bc"].ap, ALU.mult, [gv.k(tb), gm["g_bc"].key], [gv.k(tb)])
        v = vn[tb % 2]
        P.tt(v.ap, g, gm["b_bc"].ap, ALU.add, [gv.k(tb), gm["b_bc"].key], [v.key])
        for cg in range(8):
            po = P.psum()
            P.mm(po.ap[:, 0:128], v.ap[:, cg * 128:(cg + 1) * 128], gm["ws"].ap[:, cg * 128:(cg + 1) * 128], True, False,
                 [v.key, gm["ws"].key], [po.key])
            P.mm(po.ap[:, 0:128], P.ones_bf[0:1, :], gm["bs"].ap[0:1, cg * 128:(cg + 1) * 128], False, True,
                 [P.cb.key, gm["bs"].key], [po.key])
            P.tt(bo3[:, cg, tb * 128:(tb + 1) * 128], po.ap[:, 0:128], u3[:, cg, tb * 128:(tb + 1) * 128], ALU.mult,
                 [po.key, uT.k(cg, tb // 4)], [boT.k(cg, tb)])
    for cg in range(8):
        P.dma(BO.ap[cg, :, t0:t0 + TT], bo3[:, cg, :], [boT.k(cg)], [BO.k(cg, t0)])


def load_gmlp(P, d):
    gm = {}
    gm["g_bc"] = P.f32(1024, "gbc")
    gm["b_bc"] = P.f32(1024, "bbc")
    gm["ws"] = P.bf(1024, "ws")
    gm["bs"] = P.bf(1024, "bs")
    P.dma(gm["g_bc"].ap, d["gm_g"].ap, [d["gm_g"].key], [gm["g_bc"].key])
    P.dma(gm["b_bc"].ap, d["gm_b"].ap, [d["gm_b"].key], [gm["b_bc"].key])
    P.dma(gm["ws"].ap, d["gm_ws"].ap, [d["gm_ws"].key], [gm["ws"].key], eng="gpsimd")
    P.dma(gm["bs"].ap[0:1, :], d["gm_bs"].ap, [d["gm_bs"].key], [gm["bs"].key], eng="gpsimd")
    w3 = gm["ws"].ap.rearrange("p (g i) -> p g i", g=8)
    for g in range(8):
        P.tt(w3[:, g, :], w3[:, g, :], P.cb.ap[:, CB_TRI:CB_TRI + 128], ALU.mult, [gm["ws"].key, P.cb.key], [gm["ws"].key])
    return gm


def mixin1(P, xb, t0, W, QT, KT, V):
    x3 = xb.ap.rearrange("p (c t) -> p c t", c=NCH)
    slots = [P.bf(4096, "ws") for _ in range(4)]
    stg = [P.bf(512, "stg") for _ in range(4)]
    sc = [0]
    rhs = lambda kc, s: x3[:, kc, s * 512:(s + 1) * 512]
    rk = lambda kc, s: xb.k(kc, s)

    def evac_to(dst):
        def evac(i, fc, s, po):
            st = stg[sc[0] % 4]
            sc[0] += 1
            P.cp(st.ap, po.ap, [po.key], [st.key], eng=("scalar" if sc[0] % 2 else "vector"))
            P.dma(dst.ap[fc, :, t0 + s * 512:t0 + (s + 1) * 512], st.ap, [st.key], [dst.k(fc, t0, s)])
        return evac
    P.proj_fm(W, list(range(0, 8)), NCH, rhs, rk, evac_to(QT), slots)
    P.proj_fm(W, list(range(8, 16)), NCH, rhs, rk, evac_to(KT), slots)
    vbuf = P.bf(8 * 2048, "vbuf")
    vb3 = vbuf.ap.rearrange("p (b c) -> p b c", b=8)

    def evac_v(i, tb, po):
        P.cp(vb3[:, tb, i * 256:(i + 1) * 256], po.ap[:, 0:256], [po.key], [vbuf.k(tb, i)],
             eng=("scalar" if tb % 2 else "vector"))
    lhs = lambda kc, tb: x3[:, kc, tb * 128:(tb + 1) * 128]
    lk = lambda kc, tb: xb.k(kc, tb // 4)
    P.proj_tm(W, list(range(16, 24)), NCH, lhs, lk, evac_v, slots, 8)
    P.dma(V.ap[t0:t0 + TT, :].rearrange("(b p) c -> p b c", p=128), vb3, [vbuf.key], [V.k(t0)])


def mem_kv(P, mem_d, Wkv, KM, VM):
    mt = P.bf(NCH * 256, "memT")
    m3 = mt.ap.rearrange("p (c t) -> p c t", c=NCH)
    xin = [P.f32(D, "min") for _ in range(2)]
    for mb in range(2):
        xi = xin[mb]
        P.dma(xi.ap, mem_d.ap[mb * 128:(mb + 1) * 128, :], [mem_d.key], [xi.key])
        for c4 in range(4):
            po = P.psum()
            for cc in range(4):
                c = c4 * 4 + cc
                P.S.op("tensor", lambda e, po=po, cc=cc, c=c, xi=xi: e.matmul(po.ap[:, cc * 128:(cc + 1) * 128], lhsT=xi.ap[:, c * 128:(c + 1) * 128], rhs=P.identF, start=True, stop=True),
                    reads=[xi.key, P.cf.key], writes=[po.key])
            for cc in range(4):
                P.cp(m3[:, c4 * 4 + cc, mb * 128:(mb + 1) * 128], po.ap[:, cc * 128:(cc + 1) * 128],
                     [po.key], [mt.k(c4 * 4 + cc, mb)], eng=("scalar" if cc % 2 else "vector"))
    slots = [P.bf(4096, "ws") for _ in range(4)]
    km = P.bf(NCH * 256, "km")
    k3 = km.ap.rearrange("p (c t) -> p c t", c=NCH)
    vm = P.bf(2 * D, "vm")
    v3 = vm.ap.rearrange("p (b c) -> p b c", b=2)

    def evac_k(i, fc, s, po):
        P.cp(k3[:, fc, :], po.ap[:, 0:256], [po.key], [km.k(fc)], eng=("scalar" if fc % 2 else "vector"))
    P.proj_fm(Wkv, list(range(0, 8)), NCH, lambda kc, s: m3[:, kc, :], lambda kc, s: mt.key, evac_k, slots, ntile=1, nw=256)

    def evac_v(i, tb, po):
        P.cp(v3[:, tb, i * 256:(i + 1) * 256], po.ap[:, 0:256], [po.key], [vm.k(tb, i)],
             eng=("scalar" if tb % 2 else "vector"))
    P.proj_tm(Wkv, list(range(8, 16)), NCH, lambda kc, tb: m3[:, kc, tb * 128:(tb + 1) * 128], lambda kc, tb: mt.key,
              evac_v, slots, 2)
    P.dma(KM.ap, k3, [km.key], [KM.key])
    P.dma(VM.ap, v3, [vm.key], [VM.key])


def mematt(P, z, xb, Wq, Wo, KM, VM):
    x3 = xb.ap.rearrange("p (c t) -> p c t", c=NCH)
    km = P.bf(NCH * 256, "km")
    k3 = km.ap.rearrange("p (c t) -> p c t", c=NCH)
    vm = P.bf(2 * D, "vm")
    v3 = vm.ap.rearrange("p (b c) -> p b c", b=2)
    P.dma(k3, KM.ap, [KM.key], [km.key])
    P.dma(v3, VM.ap, [VM.key], [vm.key])
    slots = [P.bf(4096, "ws") for _ in range(3)]
    qT = P.bf(NCH * TT, "qT")
    q3 = qT.ap.rearrange("p (c t) -> p c t", c=NCH)

    def evac_q(i, fc, s, po):
        P.cp(q3[:, fc, s * 512:(s + 1) * 512], po.ap, [po.key], [qT.k(fc, s)], eng=("scalar" if (fc + s) % 2 else "vector"))
    P.proj_fm(Wq, list(range(8)), NCH, lambda kc, s: x3[:, kc, s * 512:(s + 1) * 512], lambda kc, s: xb.k(kc, s),
              evac_q, slots)
    pT = [P.bf(512, "pT") for _ in range(4)]
    rec = [P.f32(512, "rec") for _ in range(2)]
    it = 0
    for hd in range(4):
        for s in range(2):
            sl = slice(s * 512, (s + 1) * 512)
            pts = []
            for mb in range(2):
                po = P.psum()
                for cc in range(4):
                    c = 4 * hd + cc
                    P.mm(po.ap, k3[:, c, mb * 128:(mb + 1) * 128], q3[:, c, sl], cc == 0, cc == 3,
                         [km.key, qT.k(c, s)], [po.key])
                p = pT[(2 * it + mb) % 4]
                P.act(p.ap, po.ap, AF.Exp, [po.key], [p.key], scale=float(512 ** -0.5))
                pts.append(p)
            pd = P.psum()
            for mb in range(2):
                P.mm(pd.ap, P.ones_bf, pts[mb].ap, mb == 0, mb == 1, [P.cb.key, pts[mb].key], [pd.key])
            r = rec[it % 2]
            P.S.op("vector", lambda e, r=r, pd=pd: e.reciprocal(out=r.ap, in_=pd.ap), reads=[pd.key], writes=[r.key])
            for cc in range(4):
                c = 4 * hd + cc
                po = P.psum()
                for mb in range(2):
                    P.mm(po.ap, v3[:, mb, c * 128:(c + 1) * 128], pts[mb].ap, mb == 0, mb == 1,
                         [vm.key, pts[mb].key], [po.key])
                P.tt(x3[:, c, sl], po.ap, r.ap, ALU.mult, [po.key, r.key], [xb.k(c, s)])
            it += 1
    P.prescale(z)
    P.proj_fm(Wo, list(range(8)), NCH, lambda kc, s: x3[:, kc, s * 512:(s + 1) * 512], lambda kc, s: xb.k(kc, s),
              P.accum_z_evac(z, 1.0), slots)


def finish_head(P, oacc, otm_ap, otm_key, rd):
    P.S.op("vector", lambda e: e.reciprocal(out=rd.ap[:, 0:1], in_=oacc[0][:, 64:65]), reads=[oacc[1]], writes=[rd.key])
    P.act(otm_ap, oacc[0][:, 0:64], AF.Copy, [oacc[1], rd.key], [otm_key], scale=rd.ap[:, 0:1])


def att0(P, QT, KT, KTp, V, Vp, pv, AO):
    NT, NP = P.NT, P.NP
    nqb = NT // 128
    npb = NP // 128
    nkb = npb + nqb
    for i in range(4):
        m = P.mark()
        q3 = P.bf(3 * NT, "q3")
        k3 = P.bf(3 * (NP + NT), "k3")
        va = P.bf(nkb * 6 * 66, "va")
        qv = q3.ap.rearrange("p (g t) -> p g t", g=3)
        kv = k3.ap.rearrange("p (g t) -> p g t", g=3)
        vv = va.ap.rearrange("p (b h c) -> p b h c", b=nkb, h=6)
        for g in range(3):
            c = 4 * g + i
            P.dma(qv[:, g, :], QT.ap[c], [QT.k(c)], [q3.k(g)])
            if NP:
                P.dma(kv[:, g, 0:NP], KTp.ap[c], [KTp.k(c)], [k3.k(g, 0)])
            P.dma(kv[:, g, NP:NP + NT], KT.ap[c], [KT.k(c)], [k3.k(g, 1)])
            for hh in range(2):
                if NP:
                    P.dma(vv[:, 0:npb, 2 * g + hh, 0:64],
                          Vp.ap.rearrange("(b p) (h c) -> p b h c", p=128, c=64)[:, :, 2 * c + hh, :],
                          [Vp.key], [va.k(0, g, hh)])
                P.dma(vv[:, npb:nkb, 2 * g + hh, 0:64],
                      V.ap.rearrange("(b p) (h c) -> p b h c", p=128, c=64)[:, :, 2 * c + hh, :],
                      [V.key], [va.k(1, g, hh)])
        P.S.op("vector", lambda e, va=va: e.memset(va.ap.rearrange("p (n c) -> p n c", c=66)[:, :, 64:66], 1.0),
               reads=[], writes=[va.key])
        if NP:
            P.ts(va.ap[:, 0:npb * 6 * 66], va.ap[:, 0:npb * 6 * 66], pv.ap[:, 0:1], None, ALU.mult, None,
                 [va.key, pv.key], [va.key])
        otm = P.bf(nqb * 128, "otm")
        o3 = otm.ap.rearrange("p (b c) -> p b c", b=nqb)
        aot = P.bf(NT, "aot")
        pT = [P.bf(128, "pT") for _ in range(6)]
        rd = [P.f32(2, "rd") for _ in range(4)]
        pc = 0
        for qb in range(nqb):
            for jj in range(2):
                pr = slice(jj * 64, (jj + 1) * 64)
                ob = P.ps[6]
                osl = (qb * 2 + jj) % 4
                oacc = (ob.ap[:, osl * 128:osl * 128 + 65], ob.k(osl))
                blocks = []
                for g, d in enumerate((1, 4, 16)):
                    for dl in range(d + 1):
                        kb = npb + qb - dl
                        if kb < 0:
                            continue
                        mt = 0 if dl == 0 else (2 if dl == d else 1)
                        blocks.append((g, kb, CB_DMASK + (3 * g + mt) * 128))
                for bi, (g, kb, mo) in enumerate(blocks):
                    sb = P.ps[pc % 4]
                    ss = (pc // 4) % 4
                    sap = sb.ap[:, ss * 128:(ss + 1) * 128]
                    P.mm(sap, kv[pr, g, kb * 128:(kb + 1) * 128], qv[pr, g, qb * 128:(qb + 1) * 128], True, False,
                         [k3.k(g), q3.k(g)], [sb.k(ss)])
                    P.mm(sap, P.ident_bf, P.cb.ap[:, mo:mo + 128], False, True, [P.cb.key], [sb.k(ss)])
                    p = pT[pc % 6]
                    P.act(p.ap, sap, AF.Exp, [sb.k(ss)], [p.key], scale=0.125)
                    P.mm(oacc[0], p.ap, vv[:, kb, 2 * g + jj, 0:65], bi == 0, bi == len(blocks) - 1,
                         [p.key, va.key], [oacc[1]])
                    pc += 1
                r = rd[(qb * 2 + jj) % 4]
                finish_head(P, oacc, o3[:, qb, jj * 64:(jj + 1) * 64], otm.k(qb, jj), r)
            tp = P.ps[7]
            tsl = qb % 4
            tpb = tp.ap[:, tsl * 128:(tsl + 1) * 128]
            P.S.op("tensor", lambda e, tpb=tpb, qb=qb: e.matmul(tpb, lhsT=o3[:, qb, :], rhs=P.ident_bf, start=True, stop=True),
                   reads=[otm.k(qb), P.cb.key], writes=[tp.k(tsl)])
            P.cp(aot.ap[:, qb * 128:(qb + 1) * 128], tpb, [tp.k(tsl)], [aot.k(qb)])
        P.dma(AO.ap[i], aot.ap, [aot.key], [AO.k(i)])
        P.reset(m)


def att1(P, QT, KT, KTp, V, Vp, pv, vb_d, AO):
    NT, NP = P.NT, P.NP
    nq = NT // 256
    npb = NP // 128
    nkb = npb + NT // 128
    nb = (NP + NT) // 256
    assert nb == 16
    n0 = NP // 256
    vbias = P.f32(nq * 16, "vbias")
    P.dma(vbias.ap, vb_d.ap, [vb_d.key], [vbias.key])
    own = P.f32(nq * 16, "own")
    P.S.op("vector", lambda e: e.memset(own.ap, 0.0), writes=[own.key])
    for n in range(nq):
        P.S.op("vector", lambda e, n=n: e.memset(own.ap[:, n * 16 + n0 + n:n * 16 + n0 + n + 1], 1.0), writes=[own.key])
    m0 = P.mark()
    import os
    for c in range(int(os.environ.get("A1N", NCH))):
        P.reset(m0)
        qt = P.bf(NT, "qt")
        kt = P.bf(NP + NT, "kt")
        va = P.bf(nkb * 2 * 66, "va")
        vv = va.ap.rearrange("p (b h c) -> p b h c", b=nkb, h=2)
        P.dma(qt.ap, QT.ap[c], [QT.k(c)], [qt.key])
        if NP:
            P.dma(kt.ap[:, 0:NP], KTp.ap[c], [KTp.k(c)], [kt.k(0)])
        P.dma(kt.ap[:, NP:NP + NT], KT.ap[c], [KT.k(c)], [kt.k(1)])
        for hh in range(2):
            if NP:
                P.dma(vv[:, 0:npb, hh, 0:64], Vp.ap.rearrange("(b p) (h c) -> p b h c", p=128, c=64)[:, :, 2 * c + hh, :],
                      [Vp.key], [va.k(0, hh)])
            P.dma(vv[:, npb:nkb, hh, 0:64], V.ap.rearrange("(b p) (h c) -> p b h c", p=128, c=64)[:, :, 2 * c + hh, :],
                  [V.key], [va.k(1, hh)])
        P.S.op("vector", lambda e, va=va: e.memset(va.ap.rearrange("p (n c) -> p n c", c=66)[:, :, 64:66], 1.0),
               reads=[], writes=[va.key])
        if NP:
            P.ts(va.ap[:, 0:npb * 2 * 66], va.ap[:, 0:npb * 2 * 66], pv.ap[:, 0:1], None, ALU.mult, None,
                 [va.key, pv.key], [va.key])
        kmf = P.f32(16, "kmf")
        km = P.bf(16, "km")
        P.S.op("vector", lambda e, kt=kt, kmf=kmf: e.reduce_sum(out=kmf.ap, in_=kt.ap.rearrange("p (b t) -> p b t", t=256),
                                                                axis=AX.X), reads=[kt.key], writes=[kmf.key])
        P.ts(km.ap, kmf.ap, 1.0 / 256, None, ALU.mult, None, [kmf.key], [km.key])
        otm = P.bf(NT, "otm")
        o3 = otm.ap.rearrange("p (b c) -> p b c", c=128)
        aot = P.bf(NT, "aot")
        pT = [P.bf(256, "pT") for _ in range(4)]
        rd = [P.f32(2, "rd") for _ in range(4)]
        gmb = [P.f32(16, "gm") for _ in range(2)]
        m8 = [P.f32(8, "m8") for _ in range(2)]
        s2 = [P.f32(16, "s2") for _ in range(2)]
        selb = [P.bf(16, "selb") for _ in range(2)]
        selT = [P.bf(256, "selT") for _ in range(2)]
        pc = 0
        hc = 0
        for jj in range(2):
            pr = slice(jj * 64, (jj + 1) * 64)
            for n in range(nq):
                sT = selT[hc % 2]
                for a in range(2):
                    qsl = slice(n * 256 + a * 128, n * 256 + (a + 1) * 128)
                    pg = P.ps[5]
                    gs = (hc * 2 + a) % 4
                    gap = pg.ap[:, gs * 16:(gs + 1) * 16]
                    P.mm(gap, qt.ap[pr, qsl], km.ap[pr, :], True, True, [qt.key, km.key], [pg.k(gs)])
                    gm = gmb[a]
                    P.tt(gm.ap, gap, vbias.ap[:, n * 16:(n + 1) * 16], ALU.add, [pg.k(gs), vbias.key], [gm.key])
                    P.S.op("vector", lambda e, a=a, gm=gm: e.max(out=m8[a].ap, in_=gm.ap), reads=[gm.key], writes=[m8[a].key])
                    P.ts(s2[a].ap, gm.ap, -1e29, None, ALU.is_gt, None, [gm.key], [s2[a].key])
                    P.stt(s2[a].ap, gm.ap, m8[a].ap[:, 2:3], s2[a].ap, ALU.is_ge, ALU.mult, [gm.key, m8[a].key, s2[a].key],
                          [s2[a].key])
                    P.tt(s2[a].ap, s2[a].ap, own.ap[:, n * 16:(n + 1) * 16], ALU.add, [s2[a].key, own.key], [s2[a].key])
                    P.ts(selb[a].ap, s2[a].ap, -1.0, -NEG, ALU.add, ALU.mult, [s2[a].key], [selb[a].key])
                    tp = P.ps[7]
                    tsl = (hc * 2 + a) % 4
                    tpb = tp.ap[0:16, tsl * 128:(tsl + 1) * 128]
                    P.S.op("tensor", lambda e, tpb=tpb, a=a: e.matmul(tpb, lhsT=selb[a].ap, rhs=P.ident_bf, start=True, stop=True),
                           reads=[selb[a].key, P.cb.key], writes=[tp.k(tsl)])
                    P.cp(sT.ap[0:16, a * 128:(a + 1) * 128], tpb, [tp.k(tsl)], [sT.k(a)], eng="scalar")
                oaccs = []
                for a in range(2):
                    ob = P.ps[6 if a == 0 else 4]
                    osl = hc % 4
                    oaccs.append((ob.ap[:, osl * 128:osl * 128 + 65], ob.k(osl)))
                nks = 2 * (n0 + n + 1)
                for ks in range(nks):
                    kbi = ks // 2
                    sb = P.ps[pc % 4]
                    ss = (pc // 4) % 2
                    sap = sb.ap[:, ss * 256:(ss + 1) * 256]
                    ownb = (kbi == n0 + n)
                    P.mm(sap, kt.ap[pr, ks * 128:(ks + 1) * 128], qt.ap[pr, n * 256:(n + 1) * 256], True, False,
                         [kt.key, qt.key], [sb.k(ss)])
                    P.mm(sap, P.cb.ap[0:16, CB_E + kbi * 128:CB_E + (kbi + 1) * 128], sT.ap[0:16, :], False, not ownb,
                         [P.cb.key, sT.key], [sb.k(ss)])
                    if ownb:
                        co = CB_CAUS + (ks % 2) * 256
                        P.mm(sap, P.ident_bf, P.cb.ap[:, co:co + 256], False, True, [P.cb.key], [sb.k(ss)])
                    p = pT[pc % 4]
                    P.act(p.ap, sap, AF.Exp, [sb.k(ss)], [p.key], scale=0.125)
                    for a in range(2):
                        P.mm(oaccs[a][0], p.ap[:, a * 128:(a + 1) * 128], vv[:, ks, jj, 0:65], ks == 0, ks == nks - 1,
                             [p.key, va.key], [oaccs[a][1]])
                    pc += 1
                for a in range(2):
                    r = rd[(hc * 2 + a) % 4]
                    finish_head(P, oaccs[a], o3[:, 2 * n + a, jj * 64:(jj + 1) * 64], otm.k(2 * n + a, jj), r)
                hc += 1
        for qb in range(NT // 128):
            tp = P.ps[7]
            tsl = qb % 4
            tpb = tp.ap[:, tsl * 128:(tsl + 1) * 128]
            P.S.op("tensor", lambda e, tpb=tpb, qb=qb, o3=o3: e.matmul(tpb, lhsT=o3[:, qb, :], rhs=P.ident_bf, start=True, stop=True),
                   reads=[otm.k(qb), P.cb.key], writes=[tp.k(tsl)])
            P.cp(aot.ap[:, qb * 128:(qb + 1) * 128], tpb, [tp.k(tsl)], [aot.k(qb)])
        P.dma(AO.ap[c], aot.ap, [aot.key], [AO.k(c)])
    P.reset(m0)


NT_U = 2048


def common_begin(nc, NT, NP):
    P = Prog(nc, NT, NP)
    cb_d = P.dram_in("cb", [128, CB_N])
    cf_d = P.dram_in("cf", [128, CF_N])
    P.load_consts(cb_d, cf_d)
    return P


def build_stage0(upto=9):
    nc = bass.Bass("TRN2", target_bir_lowering=False)
    P = common_begin(nc, NT_U, 0)
    NT = P.NT
    x_d = P.dram_in("x", [NT, D])
    win = P.dram_in("ffn1_win", [NHC, 128, NCH * 256])
    wout = P.dram_in("ffn1_wout", [NHC, 128, D])
    lnp_d = P.dram_in("lnp", [128, 32])
    Wmix = P.dram_in("mixin", [26, 128, NCH * 256])
    gd = {"gm_g": P.dram_in("gm_g", [128, 1024]), "gm_b": P.dram_in("gm_b", [128, 1024]),
          "gm_ws": P.dram_in("gm_ws", [128, 1024]), "gm_bs": P.dram_in("gm_bs", [1, 1024])}
    XZ = P.dram_out("XZ", [128, NCH, NT])
    QT = P.dram_out("QT", [12, 128, NT], BF16)
    KT = P.dram_out("KT", [12, 128, NT], BF16)
    V = P.dram_out("V", [NT, 1536], BF16)
    BO = P.dram_out("BO", [8, 128, NT], BF16)
    with P.es:
        lnp = load_ln(P, lnp_d, 1)
        gm = load_gmlp(P, gd)
        xb = P.bf(NCH * TT, "xb")
        mz = P.mark()
        z = P.f32(NCH * TT, "z")
        m = P.mark()
        for tt in range(NT // TT):
            t0 = tt * TT
            P.reset(m)
            P.load_x(x_d, t0, z, xb)
            P.reset(m)
            if upto >= 2:
                P.ffn(z, xb, win, wout)
            if upto >= 3:
                g, b = ln_aps(lnp, 0)
                P.layernorm(z, xb, g, b, lnp.key)
            P.dma(XZ.ap[:, :, t0:t0 + TT], z.ap.rearrange("p (c t) -> p c t", c=NCH), [z.key], [XZ.k(t0)])
            P.reset(mz)
            if upto >= 4:
                mixin0(P, xb, t0, Wmix, gm, QT, KT, V, BO, upto)
        P.S.emit()
    return nc, P


def row_chain(P, z, xb, lnp, li, d, KM, VM):
    m = P.mark()
    mematt(P, z, xb, d["wq"], d["wo"], KM, VM)
    g, b = ln_aps(lnp, li)
    P.layernorm(z, xb, g, b, lnp.key)
    P.reset(m)
    P.ffn(z, xb, d["ffn2_win"], d["ffn2_wout"])
    g, b = ln_aps(lnp, li + 1)
    P.layernorm(z, xb, g, b, lnp.key)
    P.reset(m)


def build_stage1():
    nc = bass.Bass("TRN2", target_bir_lowering=False)
    P = common_begin(nc, NT_U, NT_U)
    NT, NP = P.NT, P.NP
    XZ = P.dram_in("XZ", [128, NCH, NT])
    QT = P.dram_in("QT", [12, 128, NT], BF16)
    KT = P.dram_in("KT", [12, 128, NT], BF16)
    KTp = P.dram_in("KTp", [12, 128, NP], BF16)
    V = P.dram_in("V", [NT, 1536], BF16)
    Vp = P.dram_in("Vp", [NP, 1536], BF16)
    BO = P.dram_in("BO", [8, 128, NT], BF16)
    pv_d = P.dram_in("pv", [128, 2])
    mem_d = P.dram_in("mem", [256, D])
    d = {"wq": P.dram_in("wq", [8, 128, NCH * 256]), "wo": P.dram_in("wo", [8, 128, NCH * 256]),
         "ffn2_win": P.dram_in("ffn2_win", [NHC, 128, NCH * 256]), "ffn2_wout": P.dram_in("ffn2_wout", [NHC, 128, D])}
    Wkv = P.dram_in("wkv", [16, 128, NCH * 256])
    Wmo = P.dram_in("mixout", [8, 128, 12 * 256])
    win1 = P.dram_in("ffn1_win", [NHC, 128, NCH * 256])
    wout1 = P.dram_in("ffn1_wout", [NHC, 128, D])
    Wmix1 = P.dram_in("mixin", [24, 128, NCH * 256])
    lnp_d = P.dram_in("lnp", [128, 32 * 4])
    XZo = P.dram_out("XZo", [128, NCH, NT])
    QTo = P.dram_out("QTo", [NCH, 128, NT], BF16)
    KTo = P.dram_out("KTo", [NCH, 128, NT], BF16)
    Vo = P.dram_out("Vo", [NT, D], BF16)
    AO = P.dram_tmp("AO", [4, 128, NT], BF16)
    KM = P.dram_tmp("KM", [128, NCH, 256], BF16)
    VM = P.dram_tmp("VM", [128, 2, D], BF16)
    with P.es:
        lnp = load_ln(P, lnp_d, 4)
        pv = P.f32(2, "pv")
        P.dma(pv.ap, pv_d.ap, [pv_d.key], [pv.key])
        m = P.mark()
        mem_kv(P, mem_d, Wkv, KM, VM)
        P.reset(m)
        att0(P, QT, KT, KTp, V, Vp, pv, AO)
        P.reset(m)
        z = P.f32(NCH * TT, "z")
        xb = P.bf(NCH * TT, "xb")
        m = P.mark()
        z3 = z.ap.rearrange("p (c t) -> p c t", c=NCH)
        x3 = xb.ap.rearrange("p (c t) -> p c t", c=NCH)
        for tt in range(NT // TT):
            t0 = tt * TT
            P.reset(m)
            P.dma(z3, XZ.ap[:, :, t0:t0 + TT], [XZ.key], [z.key])
            for c in range(4):
                P.dma(x3[:, c, :], AO.ap[c, :, t0:t0 + TT], [AO.k(c)], [xb.k(c)])
            for c in range(8):
                P.dma(x3[:, 4 + c, :], BO.ap[c, :, t0:t0 + TT], [BO.key], [xb.k(4 + c)])
            P.prescale(z)
            slots = [P.bf(4096, "ws") for _ in range(4)]
            P.proj_fm(Wmo, list(range(8)), 12, lambda kc, s: x3[:, kc, s * 512:(s + 1) * 512], lambda kc, s: xb.k(kc, s),
                      P.accum_z_evac(z, 1.0), slots)
            g, b = ln_aps(lnp, 0)
            P.layernorm(z, xb, g, b, lnp.key)
            P.reset(m)
            row_chain(P, z, xb, lnp, 1, d, KM, VM)
            P.ffn(z, xb, win1, wout1)
            g, b = ln_aps(lnp, 3)
            P.layernorm(z, xb, g, b, lnp.key)
            P.dma(XZo.ap[:, :, t0:t0 + TT], z3, [z.key], [XZo.k(t0)])
            P.reset(m)
            mixin1(P, xb, t0, Wmix1, QTo, KTo, Vo)
        P.S.emit()
    return nc, P


def build_stage2():
    nc = bass.Bass("TRN2", target_bir_lowering=False)
    P = common_begin(nc, NT_U, NT_U)
    NT, NP = P.NT, P.NP
    XZ = P.dram_in("XZ", [128, NCH, NT])
    QT = P.dram_in("QT", [NCH, 128, NT], BF16)
    KT = P.dram_in("KT", [NCH, 128, NT], BF16)
    KTp = P.dram_in("KTp", [NCH, 128, NP], BF16)
    V = P.dram_in("V", [NT, D], BF16)
    Vp = P.dram_in("Vp", [NP, D], BF16)
    pv_d = P.dram_in("pv", [128, 2])
    vb_d = P.dram_in("vb", [128, (NT // 256) * 16])
    mem_d = P.dram_in("mem", [256, D])
    d = {"wq": P.dram_in("wq", [8, 128, NCH * 256]), "wo": P.dram_in("wo", [8, 128, NCH * 256]),
         "ffn2_win": P.dram_in("ffn2_win", [NHC, 128, NCH * 256]), "ffn2_wout": P.dram_in("ffn2_wout", [NHC, 128, D])}
    Wkv = P.dram_in("wkv", [16, 128, NCH * 256])
    Wmo = P.dram_in("mixout", [8, 128, NCH * 256])
    lnp_d = P.dram_in("lnp", [128, 32 * 3])
    out_d = P.dram_out("out", [NT, D])
    AO = P.dram_tmp("AO", [NCH, 128, NT], BF16)
    KM = P.dram_tmp("KM", [128, NCH, 256], BF16)
    VM = P.dram_tmp("VM", [128, 2, D], BF16)
    with P.es:
        lnp = load_ln(P, lnp_d, 3)
        pv = P.f32(2, "pv")
        P.dma(pv.ap, pv_d.ap, [pv_d.key], [pv.key])
        m = P.mark()
        mem_kv(P, mem_d, Wkv, KM, VM)
        P.reset(m)
        att1(P, QT, KT, KTp, V, Vp, pv, vb_d, AO)
        P.reset(m)
        z = P.f32(NCH * TT, "z")
        xb = P.bf(NCH * TT, "xb")
        m = P.mark()
        z3 = z.ap.rearrange("p (c t) -> p c t", c=NCH)
        x3 = xb.ap.rearrange("p (c t) -> p c t", c=NCH)
        for tt in range(NT // TT):
            t0 = tt * TT
            P.reset(m)
            P.dma(z3, XZ.ap[:, :, t0:t0 + TT], [XZ.key], [z.key])
            for c in range(NCH):
                P.dma(x3[:, c, :], AO.ap[c, :, t0:t0 + TT], [AO.k(c)], [xb.k(c)])
            P.prescale(z)
            slots = [P.bf(4096, "ws") for _ in range(4)]
            P.proj_fm(Wmo, list(range(8)), NCH, lambda kc, s: x3[:, kc, s * 512:(s + 1) * 512], lambda kc, s: xb.k(kc, s),
                      P.accum_z_evac(z, 1.0), slots)
            g, b = ln_aps(lnp, 0)
            P.layernorm(z, xb, g, b, lnp.key)
            P.reset(m)
            row_chain(P, z, xb, lnp, 1, d, KM, VM)
            P.store_out(out_d, t0, z)
        P.S.emit()
    return nc, P


def lnpack(inp, names):
    cols = []
    for n in names:
        cols.append(lay_vec(inp[n + "_g"]))
        cols.append(lay_vec(inp[n + "_b"]))
    return np.ascontiguousarray(np.concatenate(cols, 1).astype(np.float32))


def run(nc, maps):
    res = run_bass_kernel_spmd(nc, maps, core_ids=list(range(8)))
    return res.results


def prev_of(arrs, c, axis_tok, zero_like):
    if c % 2 == 1:
        return arrs[c - 1]
    return np.zeros_like(zero_like)


def kernel(**inp):
    inp = {k: np.asarray(v) for k, v in inp.items()}
    x = inp["x"].reshape(8, NT_U, D)
    mem = inp["mem"]
    cb, cf = build_consts()
    base = {"cb": cb, "cf": cf}
    nc0, _ = build_stage0()
    w = {"ffn1_win": lay_ffn_in(inp["l0_ffn1_w_in"]), "ffn1_wout": inp["l0_ffn1_w_out"].reshape(NHC, 128, D),
         "lnp": lnpack(inp, ["l0_ln1"]), "mixin": lay_units(inp["l0_mix_w_in"]),
         "gm_g": np.ascontiguousarray(np.broadcast_to(inp["l0_gmlp_ln_g"][None, :], (128, 1024))),
         "gm_b": np.ascontiguousarray(np.broadcast_to(inp["l0_gmlp_ln_b"][None, :], (128, 1024))),
         "gm_ws": np.ascontiguousarray(inp["l0_gmlp_w_s"].transpose(2, 0, 1)).reshape(128, 1024),
         "gm_bs": np.ascontiguousarray(inp["l0_gmlp_b_s"].reshape(1, 1024))}
    r0 = run(nc0, [dict(base, x=x[c], **w) for c in range(8)])
    del w
    nc1, _ = build_stage1()
    w = {"wq": lay_units(inp["l0_mem_w_q"]), "wo": lay_units(inp["l0_mem_w_o"]), "wkv": lay_units(inp["l0_mem_w_kv"]),
         "ffn2_win": lay_ffn_in(inp["l0_ffn2_w_in"]), "ffn2_wout": inp["l0_ffn2_w_out"].reshape(NHC, 128, D),
         "mixout": lay_units(inp["l0_mix_w_out"]),
         "ffn1_win": lay_ffn_in(inp["l1_ffn1_w_in"]), "ffn1_wout": inp["l1_ffn1_w_out"].reshape(NHC, 128, D),
         "mixin": lay_units(inp["l1_mix_w_in"]),
         "lnp": lnpack(inp, ["l0_ln2", "l0_ln3", "l0_ln4", "l1_ln1"])}
    maps = []
    for c in range(8):
        pvv = np.full((128, 2), float(c % 2), np.float32)
        m = dict(base, XZ=r0[c]["XZ"], QT=r0[c]["QT"], KT=r0[c]["KT"], V=r0[c]["V"], BO=r0[c]["BO"],
                 KTp=prev_of([r["KT"] for r in r0], c, 2, r0[c]["KT"]),
                 Vp=prev_of([r["V"] for r in r0], c, 0, r0[c]["V"]),
                 pv=pvv, mem=mem[c // 2], **w)
        maps.append(m)
    r1 = run(nc1, maps)
    del w, maps, r0
    nc2, _ = build_stage2()
    w = {"wq": lay_units(inp["l1_mem_w_q"]), "wo": lay_units(inp["l1_mem_w_o"]), "wkv": lay_units(inp["l1_mem_w_kv"]),
         "ffn2_win": lay_ffn_in(inp["l1_ffn2_w_in"]), "ffn2_wout": inp["l1_ffn2_w_out"].reshape(NHC, 128, D),
         "mixout": lay_units(inp["l1_mix_w_out"]),
         "lnp": lnpack(inp, ["l1_ln2", "l1_ln3", "l1_ln4"])}
    maps = []
    for c in range(8):
        h = c % 2
        pvv = np.full((128, 2), float(h), np.float32)
        vb = np.full((128, 8, 16), -1e30, np.float32)
        for n in range(8):
            if h:
                vb[:, n, 0:8] = 0.0
            vb[:, n, 8:8 + n] = 0.0
        m = dict(base, XZ=r1[c]["XZo"], QT=r1[c]["QTo"], KT=r1[c]["KTo"], V=r1[c]["Vo"],
                 KTp=prev_of([r["KTo"] for r in r1], c, 2, r1[c]["KTo"]),
                 Vp=prev_of([r["Vo"] for r in r1], c, 0, r1[c]["Vo"]),
                 pv=pvv, vb=vb.reshape(128, 128), mem=mem[c // 2], **w)
        maps.append(m)
    r2 = run(nc2, maps)
    out = np.stack([r2[c]["out"] for c in range(8)], 0).reshape(4, 2 * NT_U, D)
    return out.astype(np.float32)
```
